# Optimizing a Trainium2 kernel written in Bass

```python
import math
import jax, jax.numpy as jnp
from jax import lax
import numpy as np

D_MODEL = 4096
BATCH = 2
SEQ = 4096
DEPTH = 2

CHUNK = 64
HEAD_DIM = 128
EPS = 1e-6
SSM_WIDTH = 1024
SSM_GROUP = 16
SSM_GROUPS = SSM_WIDTH // SSM_GROUP
SSM_STATE = 64
CA_HEADS = 12
CA_WIDTH = CA_HEADS * HEAD_DIM
CA_LEFT_CHUNKS = 8
CA_BAND = (CA_LEFT_CHUNKS + 1) * CHUNK
REL_CLIP = 128
DA_HEADS = 6
DA_WIDTH = DA_HEADS * 2 * HEAD_DIM
Q_BLOCK = 128
N_BRANCH = 3
D_FF = 4 * D_MODEL
IN_WIDTH = SSM_WIDTH + 3 * CA_WIDTH + 3 * DA_WIDTH + N_BRANCH * D_MODEL

kernel_name = "hybrid_s5_chunkattn_diffattn_gated_block"


def rms_norm(x, gain):
    x32 = x.astype(jnp.float32)
    y = x32 * lax.rsqrt(jnp.mean(x32 * x32, axis=-1, keepdims=True) + EPS)
    return (y * gain.astype(jnp.float32)).astype(x.dtype)


def alibi_slopes(n_heads):
    return 2.0 ** (-8.0 * jnp.arange(1, n_heads + 1, dtype=jnp.float32) / n_heads)


def s5_mixer(u, a_re, a_im, log_dt, b_re, b_im, c_re, c_im, d_skip, w_glu, b_glu):
    bsz, seq, _ = u.shape
    f32 = jnp.float32
    lam = lax.complex(jnp.minimum(a_re.astype(f32), -1e-4), a_im.astype(f32))
    dt = jnp.exp(log_dt.astype(f32))[:, None]
    a_bar = jnp.exp(lam * dt)
    b = lax.complex(b_re.astype(f32), b_im.astype(f32))
    b_bar = ((a_bar - 1.0) / lam)[..., None] * b
    c = lax.complex(c_re.astype(f32), c_im.astype(f32))
    ug = u.astype(f32).reshape(bsz, seq, SSM_GROUPS, SSM_GROUP)
    bu = jnp.einsum('gph,bsgh->sbgp', b_bar, ug)
    a_seq = jnp.broadcast_to(a_bar, bu.shape)

    def combine(left, right):
        a_l, b_l = left
        a_r, b_r = right
        return a_r * a_l, a_r * b_l + b_r

    _, states = lax.associative_scan(combine, (a_seq, bu), axis=0)
    y = jnp.einsum('ghp,sbgp->bsgh', c, states).real.reshape(bsz, seq, SSM_WIDTH)
    y = y + d_skip.astype(f32) * u.astype(f32)
    z = jax.nn.gelu(y).astype(u.dtype)
    return z * jax.nn.sigmoid(z @ w_glu + b_glu)


def chunk_band_attention(q, k, v, q_gain, k_gain, rel_bias):
    bsz, seq, h, dh = q.shape
    f32 = jnp.float32
    nc = seq // CHUNK
    q = rms_norm(q, q_gain) * (dh ** -0.5)
    k = rms_norm(k, k_gain)
    pad = ((0, 0), (CA_LEFT_CHUNKS * CHUNK, 0), (0, 0), (0, 0))
    k_pad = jnp.pad(k, pad).reshape(bsz, nc + CA_LEFT_CHUNKS, CHUNK, h, dh)
    v_pad = jnp.pad(v, pad).reshape(bsz, nc + CA_LEFT_CHUNKS, CHUNK, h, dh)
    band_idx = jnp.arange(nc)[:, None] + jnp.arange(CA_LEFT_CHUNKS + 1)[None, :]
    k_band = k_pad[:, band_idx].reshape(bsz, nc, CA_BAND, h, dh)
    v_band = v_pad[:, band_idx].reshape(bsz, nc, CA_BAND, h, dh)
    qc = q.reshape(bsz, nc, CHUNK, h, dh)
    scores = jnp.einsum('bnqhd,bnkhd->bhnqk', qc, k_band).astype(f32)
    rel = CA_LEFT_CHUNKS * CHUNK + jnp.arange(CHUNK)[:, None] - jnp.arange(CA_BAND)[None, :]
    rel_idx = jnp.clip(rel, -REL_CLIP, REL_CLIP) + REL_CLIP
    bias = rel_bias.astype(f32)[:, rel_idx]
    key_pos = (jnp.arange(nc)[:, None] - CA_LEFT_CHUNKS) * CHUNK + jnp.arange(CA_BAND)[None, :]
    valid = key_pos >= 0
    scores = scores + bias[None, :, None]
    scores = jnp.where(valid[None, None, :, None, :], scores, -1e30)
    p = jax.nn.softmax(scores, axis=-1).astype(v.dtype)
    out = jnp.einsum('bhnqk,bnkhd->bnqhd', p, v_band)
    return out.reshape(bsz, seq, h * dh)


def diff_attention(q, k, v, q_gain, k_gain, lam_q1, lam_k1, lam_q2, lam_k2, subln_gain, lambda_init):
    bsz, seq, h, _, dh = q.shape
    f32 = jnp.float32
    q = rms_norm(q, q_gain) * (dh ** -0.5)
    k = rms_norm(k, k_gain)
    lam = (jnp.exp(jnp.sum(lam_q1.astype(f32) * lam_k1.astype(f32)))
           - jnp.exp(jnp.sum(lam_q2.astype(f32) * lam_k2.astype(f32))) + lambda_init)
    slopes = alibi_slopes(h)
    key_pos = jnp.arange(seq)
    nb = seq // Q_BLOCK
    q_blocks = q.reshape(bsz, nb, Q_BLOCK, h, 2, dh).transpose(1, 0, 2, 3, 4, 5)

    def block(args):
        qb, start = args
        q_pos = start + jnp.arange(Q_BLOCK)
        s = jnp.einsum('bqhcd,bkhcd->bhcqk', qb, k).astype(f32)
        dist = jnp.abs(q_pos[:, None] - key_pos[None, :]).astype(f32)
        s = s - slopes[None, :, None, None, None] * dist[None, None, None]
        allowed = (key_pos[None, :] // CHUNK) <= (q_pos[:, None] // CHUNK)
        s = jnp.where(allowed, s, -1e30)
        p = jax.nn.softmax(s, axis=-1)
        w = p[:, :, 0] - lam * p[:, :, 1]
        return jnp.einsum('bhqk,bkhe->bqhe', w.astype(v.dtype), v)

    out = lax.map(block, (q_blocks, jnp.arange(nb) * Q_BLOCK))
    out = out.transpose(1, 0, 2, 3, 4).reshape(bsz, seq, h, 2 * dh)
    out = rms_norm(out, subln_gain) * (1.0 - lambda_init)
    return out.reshape(bsz, seq, h * 2 * dh)


def setup_inputs(seed: int = 0) -> dict:
    key = jax.random.key(seed)
    ks = jax.random.split(key, 32)
    f32 = jnp.float32
    nrm = lambda k, shape, scale: jax.random.normal(k, shape, f32) * scale
    L, G, P = DEPTH, SSM_GROUPS, SSM_STATE
    return {
        "x": jax.random.normal(ks[0], (BATCH, SEQ, D_MODEL), f32),
        "norm_mix": 1.0 + nrm(ks[1], (L, D_MODEL), 0.02),
        "w_in": nrm(ks[2], (L, D_MODEL, IN_WIDTH), D_MODEL ** -0.5),
        "ssm_a_re": -0.5 + nrm(ks[3], (L, G, P), 0.01),
        "ssm_a_im": jnp.pi * jnp.arange(P, dtype=f32)[None, None, :] + nrm(ks[4], (L, G, P), 0.01),
        "ssm_log_dt": jax.random.uniform(ks[5], (L, G), f32, math.log(1e-3), math.log(1e-1)),
        "ssm_b_re": nrm(ks[6], (L, G, P, SSM_GROUP), (2 * SSM_GROUP) ** -0.5),
        "ssm_b_im": nrm(ks[7], (L, G, P, SSM_GROUP), (2 * SSM_GROUP) ** -0.5),
        "ssm_c_re": nrm(ks[8], (L, G, SSM_GROUP, P), (2 * P) ** -0.5),
        "ssm_c_im": nrm(ks[9], (L, G, SSM_GROUP, P), (2 * P) ** -0.5),
        "ssm_d": nrm(ks[10], (L, SSM_WIDTH), 1.0),
        "ssm_w_glu": nrm(ks[11], (L, SSM_WIDTH, SSM_WIDTH), SSM_WIDTH ** -0.5),
        "ssm_b_glu": nrm(ks[12], (L, SSM_WIDTH), 0.02),
        "ca_q_gain": 1.0 + nrm(ks[13], (L, HEAD_DIM), 0.02),
        "ca_k_gain": 1.0 + nrm(ks[14], (L, HEAD_DIM), 0.02),
        "ca_rel_bias": nrm(ks[15], (L, CA_HEADS, 2 * REL_CLIP + 1), 0.1),
        "da_q_gain": 1.0 + nrm(ks[16], (L, HEAD_DIM), 0.02),
        "da_k_gain": 1.0 + nrm(ks[17], (L, HEAD_DIM), 0.02),
        "da_lam_q1": nrm(ks[18], (L, HEAD_DIM), 0.1),
        "da_lam_k1": nrm(ks[19], (L, HEAD_DIM), 0.1),
        "da_lam_q2": nrm(ks[20], (L, HEAD_DIM), 0.1),
        "da_lam_k2": nrm(ks[21], (L, HEAD_DIM), 0.1),
        "da_subln_gain": 1.0 + nrm(ks[22], (L, 2 * HEAD_DIM), 0.02),
        "w_out_a": nrm(ks[23], (L, SSM_WIDTH, D_MODEL), SSM_WIDTH ** -0.5),
        "w_out_b": nrm(ks[24], (L, CA_WIDTH, D_MODEL), CA_WIDTH ** -0.5),
        "w_out_c": nrm(ks[25], (L, DA_WIDTH, D_MODEL), DA_WIDTH ** -0.5),
        "w_o": nrm(ks[26], (L, D_MODEL, D_MODEL), D_MODEL ** -0.5),
        "norm_mlp": 1.0 + nrm(ks[27], (L, D_MODEL), 0.02),
        "w_ff1": nrm(ks[28], (L, D_MODEL, D_FF), D_MODEL ** -0.5),
        "w_ff2": nrm(ks[29], (L, D_FF, D_MODEL), D_FF ** -0.5),
    }


def reference(x, norm_mix, w_in, ssm_a_re, ssm_a_im, ssm_log_dt, ssm_b_re, ssm_b_im, ssm_c_re, ssm_c_im,
              ssm_d, ssm_w_glu, ssm_b_glu, ca_q_gain, ca_k_gain, ca_rel_bias, da_q_gain, da_k_gain,
              da_lam_q1, da_lam_k1, da_lam_q2, da_lam_k2, da_subln_gain, w_out_a, w_out_b, w_out_c,
              w_o, norm_mlp, w_ff1, w_ff2):
    bsz, seq, _ = x.shape
    widths = [SSM_WIDTH, CA_WIDTH, CA_WIDTH, CA_WIDTH, DA_WIDTH, DA_WIDTH, DA_WIDTH]
    split_points = [int(v) for v in np.cumsum(widths)]
    for l in range(DEPTH):
        lambda_init = 0.8 - 0.6 * math.exp(-0.3 * l)
        h = rms_norm(x, norm_mix[l])
        proj = h @ w_in[l]
        u_a, q_b, k_b, v_b, q_c, k_c, v_c, gates = jnp.split(proj, split_points, axis=-1)
        y_a = s5_mixer(u_a, ssm_a_re[l], ssm_a_im[l], ssm_log_dt[l], ssm_b_re[l], ssm_b_im[l],
                       ssm_c_re[l], ssm_c_im[l], ssm_d[l], ssm_w_glu[l], ssm_b_glu[l])
        y_b = chunk_band_attention(q_b.reshape(bsz, seq, CA_HEADS, HEAD_DIM),
                                   k_b.reshape(bsz, seq, CA_HEADS, HEAD_DIM),
                                   v_b.reshape(bsz, seq, CA_HEADS, HEAD_DIM),
                                   ca_q_gain[l], ca_k_gain[l], ca_rel_bias[l])
        y_c = diff_attention(q_c.reshape(bsz, seq, DA_HEADS, 2, HEAD_DIM),
                             k_c.reshape(bsz, seq, DA_HEADS, 2, HEAD_DIM),
                             v_c.reshape(bsz, seq, DA_HEADS, 2 * HEAD_DIM),
                             da_q_gain[l], da_k_gain[l], da_lam_q1[l], da_lam_k1[l],
                             da_lam_q2[l], da_lam_k2[l], da_subln_gain[l], lambda_init)
        g = jax.nn.sigmoid(gates.reshape(bsz, seq, N_BRANCH, D_MODEL))
        merged = (g[:, :, 0] * (y_a @ w_out_a[l])
                  + g[:, :, 1] * (y_b @ w_out_b[l])
                  + g[:, :, 2] * (y_c @ w_out_c[l]))
        x = x + merged @ w_o[l]
        h = rms_norm(x, norm_mlp[l])
        x = x + jnp.square(jax.nn.relu(h @ w_ff1[l])) @ w_ff2[l]
    return x
```

```python
import contextlib, math
import numpy as np
import concourse.bass as bass
import concourse.mybir as mybir
from concourse.bass_utils import run_bass_kernel_spmd


F32 = mybir.dt.float32
BF16 = mybir.dt.bfloat16
ALU = mybir.AluOpType
AF = mybir.ActivationFunctionType

ENGS = ("tensor", "vector", "scalar", "gpsimd", "sync")
SAME_ENGINE_SYNC = True


class Op:
    __slots__ = ("eng", "emit", "reads", "writes", "dma_key", "idx", "waits",
                 "signal", "sig_idx", "dma_cum")

    def __init__(self, eng, emit, reads, writes, dma_key):
        self.eng = eng
        self.emit = emit
        self.reads = reads
        self.writes = writes
        self.dma_key = dma_key
        self.waits = []
        self.signal = False
        self.sig_idx = 0
        self.dma_cum = 0


class Prog:
    def __init__(self, nc):
        self.nc = nc
        self.ops = {e: [] for e in ENGS}
        self.last_w = {}
        self.readers = {}
        self.dma_counts = {}

    def _add(self, eng, emit, reads, writes, dma_key=None):
        op = Op(eng, emit, tuple(reads), tuple(writes), dma_key)
        deps = []
        for k in op.reads:
            w = self.last_w.get(k)
            if w is not None:
                deps.append(w)
        for k in op.writes:
            w = self.last_w.get(k)
            if w is not None:
                deps.append(w)
            for r in self.readers.get(k, ()):
                deps.append(r)
        seen = set()
        for d in deps:
            if d is op or id(d) in seen:
                continue
            seen.add(id(d))
            if d.dma_key is None and d.eng == eng:
                if eng == "tensor" or not SAME_ENGINE_SYNC:
                    continue
            op.waits.append(d)
            if d.dma_key is None:
                d.signal = True
        for k in op.reads:
            self.readers.setdefault(k, []).append(op)
        for k in op.writes:
            self.last_w[k] = op
            self.readers[k] = []
        if dma_key is not None:
            c = self.dma_counts.get(dma_key, 0) + 16
            self.dma_counts[dma_key] = c
            op.dma_cum = c
        self.ops[eng].append(op)
        return op

    def op(self, eng, fn, reads=(), writes=()):
        return self._add(eng, fn, reads, writes)

    def dma(self, eng, out, in_, key, reads=(), writes=(), **kw):
        def emit(e):
            return e.dma_start(out=out, in_=in_, **kw)
        return self._add(eng, emit, reads, writes, dma_key=key)

    def finish(self, final_waits=()):
        nc = self.nc
        for e in ENGS:
            n = 0
            for op in self.ops[e]:
                if op.dma_key is None and op.signal:
                    n += 1
                    op.sig_idx = n
        import contextlib
        with contextlib.ExitStack() as st:
            esem = {e: st.enter_context(nc.semaphore("s_" + e)) for e in ENGS}
            dsem = {k: st.enter_context(nc.semaphore("d_%d" % i))
                    for i, k in enumerate(self.dma_counts)}
            block = st.enter_context(nc.Block())

            def run(e, h):
                waited = {}
                for op in self.ops[e]:
                    for d in op.waits:
                        if d.dma_key is not None:
                            sem, v = dsem[d.dma_key], d.dma_cum
                            wk = ("d", d.dma_key)
                        else:
                            sem, v = esem[d.eng], d.sig_idx
                            wk = ("e", d.eng)
                        if waited.get(wk, 0) >= v:
                            continue
                        waited[wk] = v
                        h.wait_ge(sem, v)
                    ins = op.emit(h)
                    if op.dma_key is not None:
                        ins.then_inc(dsem[op.dma_key], 16)
                    elif op.signal:
                        ins.then_inc(esem[e], 1)
                if e == "sync":
                    for d in final_waits:
                        if d.dma_key is not None:
                            h.wait_ge(dsem[d.dma_key], d.dma_cum)
                        else:
                            h.wait_ge(esem[d.eng], d.sig_idx)

            @block.tensor
            def _(h):
                run("tensor", h)

            @block.vector
            def _(h):
                run("vector", h)

            @block.scalar
            def _(h):
                run("scalar", h)

            @block.gpsimd
            def _(h):
                run("gpsimd", h)

            @block.sync
            def _(h):
                run("sync", h)


T = 1024
D = 4096
KT = 32
EPS = 1e-6
IN_W = 22528


def emit_rmsnorm(nc, P, st, xT, gain_sb, hT, ones_bf, ps_pair, tag):
    xt = [st.enter_context(nc.sbuf_tensor(tag + "xt%d" % i, [128, T], F32)) for i in range(2)]
    sq = [st.enter_context(nc.sbuf_tensor(tag + "sq%d" % i, [128, T], BF16)) for i in range(2)]
    rstd = st.enter_context(nc.sbuf_tensor(tag + "rstd", [128, T], F32))
    xv = xT.rearrange("(kt p) t -> p kt t", p=128)
    for kt in range(KT):
        s = kt % 2
        P.dma("sync", xt[s][:], xv[:, kt, :], key=tag + "xt%d" % s, writes=[(tag, "xt", s)])
        P.op("scalar", lambda e, s=s: e.activation(out=sq[s][:], in_=xt[s][:], func=AF.Square),
             reads=[(tag, "xt", s)], writes=[(tag, "sq", s)])
        for hf in range(2):
            P.op("tensor", lambda e, s=s, hf=hf, kt=kt: e.matmul(ps_pair[hf][:], ones_bf[:], sq[s][:, hf*512:(hf+1)*512],
                                                          start=(kt == 0), stop=(kt == KT-1)),
                 reads=[(tag, "sq", s)], writes=[("ps", ps_pair[hf].name)])
    for hf in range(2):
        P.op("scalar", lambda e, hf=hf: e.activation(out=rstd[:, hf*512:(hf+1)*512], in_=ps_pair[hf][:], func=AF.Sqrt,
                                                     bias=EPS, scale=1.0/D),
             reads=[("ps", ps_pair[hf].name)], writes=[(tag, "rstd", hf)])
        P.op("vector", lambda e, hf=hf: e.reciprocal(out=rstd[:, hf*512:(hf+1)*512], in_=rstd[:, hf*512:(hf+1)*512]),
             reads=[(tag, "rstd", hf)], writes=[(tag, "rstd", hf)])
    for kt in range(KT):
        s = kt % 2
        P.dma("sync", xt[s][:], xv[:, kt, :], key=tag + "xt%d" % s, writes=[(tag, "xt", s)])
        P.op("vector", lambda e, s=s, kt=kt: e.scalar_tensor_tensor(out=hT[:, kt, :], in0=xt[s][:], scalar=gain_sb[:, kt:kt+1],
                                                             in1=rstd[:], op0=ALU.mult, op1=ALU.mult),
             reads=[(tag, "xt", s), (tag, "rstd", 0), (tag, "rstd", 1), "gain"], writes=[("hT", kt // 8)])


def emit_phase_a(nc, P, st, xT, gain, w_in, qkg, o):
    hT = st.enter_context(nc.sbuf_tensor("hT", [128, KT, T], BF16))
    ones_bf = st.enter_context(nc.sbuf_tensor("ones_bf", [128, 128], BF16))
    gain_sb = st.enter_context(nc.sbuf_tensor("gain_sb", [128, KT], F32))
    qkg_sb = st.enter_context(nc.sbuf_tensor("qkg_sb", [128, 4], F32))
    wp = [st.enter_context(nc.sbuf_tensor("wp%d" % i, [128, KT, 512], BF16)) for i in range(2)]
    ps = [st.enter_context(nc.psum_tensor("ps%d" % i, [128, 512], F32)) for i in range(8)]
    NST = 4
    stg = [st.enter_context(nc.sbuf_tensor("stg%d" % i, [128, 512], F32)) for i in range(NST)]
    stb = [st.enter_context(nc.sbuf_tensor("stb%d" % i, [128, 512], BF16)) for i in range(NST)]
    sqh = [st.enter_context(nc.sbuf_tensor("sqh%d" % i, [128, 512], BF16)) for i in range(2)]
    rsh = [st.enter_context(nc.sbuf_tensor("rsh%d" % i, [128, 512], F32)) for i in range(2)]

    P.op("vector", lambda e: e.memset(ones_bf[:], 1.0), writes=["ones"])
    P.dma("sync", gain_sb[:], gain, key="gain", writes=["gain"])
    P.dma("sync", qkg_sb[:], qkg, key="qkg", writes=["qkg"])
    for c in (0, 2):
        P.op("vector", lambda e, c=c: e.tensor_scalar_mul(out=qkg_sb[:, c:c+1], in0=qkg_sb[:, c:c+1], scalar1=128.0 ** -0.5),
             reads=["qkg"], writes=["qkg"])

    emit_rmsnorm(nc, P, st, xT, gain_sb, hT, ones_bf, (ps[6], ps[7]), "n1")

    wv = w_in.rearrange("(kt p) c -> p kt c", p=128)
    panels = []
    for i in range(2):
        panels.append(("fm", i, None))
    for nm, gc in (("qb", 0), ("kb", 1)):
        for i in range(3):
            panels.append(("qk", i, (nm, gc)))
    for i in range(3):
        panels.append(("vt", i, "vb"))
    for nm, gc in (("qc", 2), ("kc", 3)):
        for i in range(3):
            panels.append(("qk", i, (nm, gc)))
    for i in range(3):
        panels.append(("vt", i, "vc"))
    for i in range(24):
        panels.append(("gate", i, None))
    assert len(panels) == 44

    state = {"acc": 0, "stg": 0, "nrm": 0}
    pending = []
    outs = []

    def flush_one():
        if pending:
            pending.pop(0)()

    def mm_group(s, bank, mk_args):
        for kt in range(KT):
            lhsT, rhs = mk_args(kt)
            P.op("tensor", lambda e, lhsT=lhsT, rhs=rhs, kt=kt, bank=bank: e.matmul(ps[bank][:], lhsT, rhs, start=(kt == 0), stop=(kt == KT-1)),
                 reads=[("wp", s, kt // 16), ("hT", kt // 8)], writes=[("ps", ps[bank].name)])

    for pn, (kind, idx, info) in enumerate(panels):
        s = pn % 2
        for q in range(2):
            P.dma("gpsimd", wp[s][:, q*16:(q+1)*16, :], wv[:, q*16:(q+1)*16, pn*512:(pn+1)*512],
                  key="wp%d_%d" % (s, q), writes=[("wp", s, q)])
        if kind == "vt":
            dst = o[info]
            for tt in range(8):
                bank = state["acc"] % 4
                state["acc"] += 1
                mm_group(s, bank, lambda kt, tt=tt, s=s: (hT[:, kt, tt*128:(tt+1)*128], wp[s][:, kt, :]))
                flush_one()

                def post(bank=bank, tt=tt, idx=idx, dst=dst):
                    b = state["stg"] % NST
                    state["stg"] += 1
                    P.op("scalar", lambda e: e.copy(out=stb[b][:], in_=ps[bank][:]),
                         reads=[("ps", ps[bank].name)], writes=[("stb", b)])
                    outs.append(P.dma("sync", dst[tt*128:(tt+1)*128, idx*512:(idx+1)*512], stb[b][:], key="stb%d" % b, reads=[("stb", b)]))
                pending.append(post)
        else:
            for m in range(4):
                for hf in range(2):
                    bank = state["acc"] % 4
                    state["acc"] += 1
                    mm_group(s, bank, lambda kt, m=m, hf=hf, s=s: (wp[s][:, kt, m*128:(m+1)*128], hT[:, kt, hf*512:(hf+1)*512]))
                    flush_one()
                    if kind == "fm":
                        def post(bank=bank, m=m, hf=hf, idx=idx):
                            b = state["stg"] % NST
                            state["stg"] += 1
                            P.op("vector", lambda e: e.tensor_copy(out=stg[b][:], in_=ps[bank][:]),
                                 reads=[("ps", ps[bank].name)], writes=[("stg", b)])
                            r0 = idx*512 + m*128
                            outs.append(P.dma("sync", o["uT"][r0:r0+128, hf*512:(hf+1)*512], stg[b][:], key="stg%d" % b, reads=[("stg", b)]))
                    elif kind == "gate":
                        def post(bank=bank, m=m, hf=hf, idx=idx):
                            b = state["stg"] % NST
                            state["stg"] += 1
                            P.op("scalar", lambda e: e.activation(out=stb[b][:], in_=ps[bank][:], func=AF.Sigmoid),
                                 reads=[("ps", ps[bank].name)], writes=[("stb", b)])
                            r0 = idx*512 + m*128
                            outs.append(P.dma("sync", o["gT"][r0:r0+128, hf*512:(hf+1)*512], stb[b][:], key="stb%d" % b, reads=[("stb", b)]))
                    else:
                        def post(bank=bank, m=m, hf=hf, idx=idx, info=info):
                            nm, gc = info
                            n = state["nrm"] % 2
                            state["nrm"] += 1
                            nb = 4 + n
                            P.op("scalar", lambda e: e.activation(out=sqh[n][:], in_=ps[bank][:], func=AF.Square),
                                 reads=[("ps", ps[bank].name)], writes=[("sqh", n)])
                            P.op("tensor", lambda e: e.matmul(ps[nb][:], ones_bf[:], sqh[n][:], start=True, stop=True),
                                 reads=[("sqh", n), "ones"], writes=[("ps", ps[nb].name)])
                            P.op("scalar", lambda e: e.activation(out=rsh[n][:], in_=ps[nb][:], func=AF.Sqrt, bias=EPS, scale=1.0/128),
                                 reads=[("ps", ps[nb].name)], writes=[("rsh", n)])
                            P.op("vector", lambda e: e.reciprocal(out=rsh[n][:], in_=rsh[n][:]),
                                 reads=[("rsh", n)], writes=[("rsh", n)])
                            b = state["stg"] % NST
                            state["stg"] += 1
                            P.op("vector", lambda e: e.scalar_tensor_tensor(out=stb[b][:], in0=ps[bank][:], scalar=qkg_sb[:, gc:gc+1],
                                                                            in1=rsh[n][:], op0=ALU.mult, op1=ALU.mult),
                                 reads=[("ps", ps[bank].name), ("rsh", n), "qkg"], writes=[("stb", b)])
                            hd = idx*4 + m
                            outs.append(P.dma("sync", o[nm][hd, :, hf*512:(hf+1)*512], stb[b][:], key="stb%d" % b, reads=[("stb", b)]))
                    pending.append(post)
    while pending:
        flush_one()
    return outs


T = 1024
EPS = 1e-6
NEG = -30000.0


def emit_attn(nc, P, st, i_, o, lambda_init):
    ones_bf = st.enter_context(nc.sbuf_tensor("b_ones", [128, 128], BF16))
    ident = st.enter_context(nc.sbuf_tensor("b_ident", [128, 128], BF16))
    psS = st.enter_context(nc.psum_tensor("psS", [128, 1024], F32))
    psO = [st.enter_context(nc.psum_tensor("psO%d" % a, [128, 512], F32)) for a in range(2)]
    psD = st.enter_context(nc.psum_tensor("psD", [128, 512], F32))
    psN = st.enter_context(nc.psum_tensor("psN", [128, 512], F32))
    outs = []

    P.op("vector", lambda e: e.memset(ones_bf[:], 1.0), writes=["ones"])
    P.dma("gpsimd", ident[:], i_["ident"], key="ident", writes=["ident"])

    bm = st.enter_context(nc.sbuf_tensor("ca_bm_sb", [128, 60, 128], BF16))
    vm = st.enter_context(nc.sbuf_tensor("ca_vm_sb", [128, 12, 128], BF16))
    vext = st.enter_context(nc.sbuf_tensor("ca_v", [128, 12, 1536], BF16))
    kq = [st.enter_context(nc.sbuf_tensor("ca_kq%d" % s, [128, 1536 + 1024], BF16)) for s in range(2)]
    pT = [st.enter_context(nc.sbuf_tensor("pT%d" % s, [128, 1024], BF16)) for s in range(2)]
    rd = [st.enter_context(nc.sbuf_tensor("rd%d" % s, [128, 512], F32)) for s in range(2)]
    yst = [st.enter_context(nc.sbuf_tensor("yst%d" % s, [128, 1024], BF16)) for s in range(2)]
    for q in range(4):
        P.dma("gpsimd", bm[:, q*15:(q+1)*15, :], i_["ca_bm"][:, q*15:(q+1)*15, :], key="cabm", writes=[("cabm", q)])
    P.dma("gpsimd", vm[:], i_["ca_vm"], key="cavm", writes=["cavm"])
    vsrc = i_["vb_ext"].rearrange("(j p) c -> p j c", p=128)
    for q in range(4):
        P.dma("sync", vext[:, q*3:(q+1)*3, :], vsrc[:, q*3:(q+1)*3, :], key="cav", writes=[("cav", q)])
    cnt = 0
    for h in range(12):
        s = h % 2
        P.dma("sync", kq[s][:, 0:1536], i_["kb_ext"][h], key="cak%d" % s, writes=[("cakq", s, 0)])
        P.dma("sync", kq[s][:, 1536:2560], i_["qb"][h], key="caq%d" % s, writes=[("cakq", s, 1)])
        for i in range(8):
            for n, d in enumerate((4, 3, 2, 1, 0)):
                jt = i + 4 - d
                off = n * 128
                P.op("tensor", lambda e, s=s, jt=jt, i=i, off=off: e.matmul(psS[:, off:off+128], kq[s][:, jt*128:(jt+1)*128],
                                                                    kq[s][:, 1536 + i*128:1536 + (i+1)*128], start=True, stop=False),
                     reads=[("cakq", s, 0), ("cakq", s, 1)], writes=[("psS", n // 4)])
                P.op("tensor", lambda e, h=h, d=d, off=off: e.matmul(psS[:, off:off+128], ident[:], bm[:, h*5 + d, :], start=False, stop=True),
                     reads=["ident", ("cabm", (h*5 + d) // 15)], writes=[("psS", n // 4)])
            ps_ = cnt % 2
            cnt += 1
            P.op("scalar", lambda e, ps_=ps_: e.activation(out=pT[ps_][:, 0:512], in_=psS[:, 0:512], func=AF.Exp),
                 reads=[("psS", 0)], writes=[("pT", ps_, 0)])
            P.op("scalar", lambda e, ps_=ps_: e.activation(out=pT[ps_][:, 512:640], in_=psS[:, 512:640], func=AF.Exp),
                 reads=[("psS", 1)], writes=[("pT", ps_, 1)])
            for n in range(5):
                jt = i + n
                P.op("tensor", lambda e, ps_=ps_, n=n, jt=jt, h=h: e.matmul(psO[0][:, 0:128], vext[:, jt, h*128:(h+1)*128], pT[ps_][:, n*128:(n+1)*128],
                                                                     start=(n == 0), stop=(n == 4)),
                     reads=[("pT", ps_, n // 4), ("cav", jt // 3)], writes=[("psO", 0)])
            for n in range(5):
                jt = i + n
                P.op("tensor", lambda e, ps_=ps_, n=n, jt=jt: e.matmul(psD[:, 0:128], vm[:, jt, :], pT[ps_][:, n*128:(n+1)*128],
                                                               start=(n == 0), stop=(n == 4)),
                     reads=[("pT", ps_, n // 4), "cavm"], writes=["psD"])
            P.op("vector", lambda e, ps_=ps_: e.reciprocal(out=rd[ps_][:, 0:128], in_=psD[:, 0:128]),
                 reads=["psD"], writes=[("rd", ps_)])
            P.op("vector", lambda e, ps_=ps_, s=s, i=i: e.tensor_tensor(out=yst[s][:, i*128:(i+1)*128], in0=psO[0][:, 0:128], in1=rd[ps_][:, 0:128], op=ALU.mult),
                 reads=[("psO", 0), ("rd", ps_)], writes=[("yst", s)])
        outs.append(P.dma("sync", o["ybT"][h*128:(h+1)*128, :], yst[s][:], key="yst%d" % s, reads=[("yst", s)]))

    bcol = st.enter_context(nc.sbuf_tensor("da_bcol_sb", [128, 6 * 8 * 32], F32))
    dbm_f = st.enter_context(nc.sbuf_tensor("da_bm_f", [128, 6, 128], F32))
    dbm_t = st.enter_context(nc.sbuf_tensor("da_bm_t", [128, 6, 128], F32))
    dbm_hi = st.enter_context(nc.sbuf_tensor("da_bm_hi", [128, 6, 128], BF16))
    dbm_lo = st.enter_context(nc.sbuf_tensor("da_bm_lo", [128, 6, 128], BF16))
    lamv = st.enter_context(nc.sbuf_tensor("da_lamv", [128, 4, 128], F32))
    lamt = st.enter_context(nc.sbuf_tensor("da_lamt", [128, 2, 128], F32))
    lams = st.enter_context(nc.sbuf_tensor("da_lams", [128, 4], F32))
    gsub = st.enter_context(nc.sbuf_tensor("da_gsub_sb", [128, 2], F32))
    kf = [st.enter_context(nc.sbuf_tensor("da_kf%d" % s, [128, 2, 4096], BF16)) for s in range(2)]
    ko = [st.enter_context(nc.sbuf_tensor("da_ko%d" % s, [128, 2, 1024], BF16)) for s in range(2)]
    qo = [st.enter_context(nc.sbuf_tensor("da_qo%d" % s, [128, 2, 1024], BF16)) for s in range(2)]
    vf = [st.enter_context(nc.sbuf_tensor("da_vf%d" % s, [128, 32, 256], BF16)) for s in range(2)]
    vo = [st.enter_context(nc.sbuf_tensor("da_vo%d" % s, [128, 8, 256], BF16)) for s in range(2)]
    dpT = [st.enter_context(nc.sbuf_tensor("dpT%d" % s, [128, 512], BF16)) for s in range(4)]
    o0 = [st.enter_context(nc.sbuf_tensor("da_o0%d" % a, [128, 512], F32)) for a in range(2)]
    yy = [st.enter_context(nc.sbuf_tensor("da_yy%d" % a, [128, 512], F32)) for a in range(2)]
    tt_ = st.enter_context(nc.sbuf_tensor("da_tt", [128, 512], F32))
    sqd = [st.enter_context(nc.sbuf_tensor("da_sq%d" % a, [128, 512], BF16)) for a in range(2)]
    rsd = st.enter_context(nc.sbuf_tensor("da_rs", [128, 512], F32))
    rdd = st.enter_context(nc.sbuf_tensor("da_rdd", [128, 512], F32))
    ycs = [st.enter_context(nc.sbuf_tensor("da_ycs%d" % a, [128, 512], BF16)) for a in range(4)]

    P.dma("sync", bcol[:], i_["da_bcol"], key="dabcol", writes=["dabcol"])
    P.dma("sync", dbm_f[:], i_["da_bm"], key="dabm", writes=["dabmf"])
    P.dma("sync", gsub[:], i_["da_gsub"], key="dagsub", writes=["gsub"])
    P.dma("sync", lamv[:], i_["da_lam"].partition_broadcast(128), key="dalam", writes=["lamv"])
    P.op("vector", lambda e: e.tensor_copy(out=dbm_hi[:], in_=dbm_f[:]), reads=["dabmf"], writes=["dabmhi"])
    P.op("vector", lambda e: e.tensor_copy(out=dbm_t[:], in_=dbm_hi[:]), reads=["dabmhi"], writes=["dabmt"])
    P.op("vector", lambda e: e.tensor_tensor(out=dbm_lo[:], in0=dbm_f[:], in1=dbm_t[:], op=ALU.subtract), reads=["dabmf", "dabmt"], writes=["dabmlo"])
    for k in range(2):
        P.op("vector", lambda e, k=k: e.tensor_tensor(out=lamt[:, k, :], in0=lamv[:, 2*k, :], in1=lamv[:, 2*k+1, :], op=ALU.mult),
             reads=["lamv"], writes=[("lamt", k)])
        P.op("vector", lambda e, k=k: e.reduce_sum(out=lams[:, k:k+1], in_=lamt[:, k, :], axis=mybir.AxisListType.X),
             reads=[("lamt", k)], writes=[("lams", k)])
        P.op("scalar", lambda e, k=k: e.activation(out=lams[:, k:k+1], in_=lams[:, k:k+1], func=AF.Exp),
             reads=[("lams", k)], writes=[("lams", k)])
    P.op("vector", lambda e: e.tensor_tensor(out=lams[:, 2:3], in0=lams[:, 1:2], in1=lams[:, 0:1], op=ALU.subtract),
         reads=[("lams", 0), ("lams", 1)], writes=[("lams", 2)])
    P.op("vector", lambda e: e.tensor_scalar_add(out=lams[:, 3:4], in0=lams[:, 2:3], scalar1=-float(lambda_init)),
         reads=[("lams", 2)], writes=["nlam"])
    P.op("vector", lambda e: e.tensor_scalar_mul(out=gsub[:], in0=gsub[:], scalar1=float(1.0 - lambda_init)),
         reads=["gsub"], writes=["gsub"])

    kfull = i_["kc_full"]
    vfull = i_["vc_full"].rearrange("(j p) c -> p j c", p=128)
    vown = i_["vc_own"].rearrange("(j p) c -> p j c", p=128)
    pcnt = 0
    ycnt = 0
    for h in range(6):
        s = h % 2
        for c in range(2):
            P.dma("sync", kf[s][:, c, :], kfull[2*h + c], key="dakf%d" % s, writes=[("dakf", s, c)])
            P.dma("sync", ko[s][:, c, :], i_["kc_own"][2*h + c], key="dako%d" % s, writes=[("dako", s, c)])
            P.dma("sync", qo[s][:, c, :], i_["qc"][2*h + c], key="daqo%d" % s, writes=[("daqo", s, c)])
        P.dma("sync", vf[s][:], vfull[:, :, h*256:(h+1)*256], key="davf%d" % s, writes=[("davf", s)])
        P.dma("sync", vo[s][:], vown[:, :, h*256:(h+1)*256], key="davo%d" % s, writes=[("davo", s)])
        for qh in range(2):
            njt = 28 if qh == 0 else 32
            for c in range(2):
                first = True
                for j in range(njt):
                    sb = j % 2
                    P.op("tensor", lambda e, s=s, c=c, j=j, qh=qh, sb=sb: e.matmul(psS[:, sb*512:(sb+1)*512], kf[s][:, c, j*128:(j+1)*128],
                                                                             qo[s][:, c, qh*512:(qh+1)*512], start=True, stop=True),
                         reads=[("dakf", s, c), ("daqo", s, c)], writes=[("psS", sb)])
                    pb = pcnt % 4
                    pcnt += 1
                    for il in range(4):
                        i = qh*4 + il
                        col = (h*8 + i)*32 + j
                        P.op("scalar", lambda e, pb=pb, sb=sb, il=il, col=col: e.activation(out=dpT[pb][:, il*128:(il+1)*128],
                                                                                    in_=psS[:, sb*512 + il*128: sb*512 + (il+1)*128],
                                                                                    func=AF.Exp, bias=bcol[:, col:col+1]),
                             reads=[("psS", sb), "dabcol"], writes=[("dpT", pb)])
                    for a in range(2):
                        P.op("tensor", lambda e, a=a, s=s, j=j, pb=pb, first=first: e.matmul(psO[a][:], vf[s][:, j, a*128:(a+1)*128], dpT[pb][:],
                                                                                     start=first, stop=False),
                             reads=[("davf", s), ("dpT", pb)], writes=[("psO", a)])
                    P.op("tensor", lambda e, pb=pb, first=first: e.matmul(psD[:], ones_bf[:], dpT[pb][:], start=first, stop=False),
                         reads=["ones", ("dpT", pb)], writes=["psD"])
                    first = False
                for il in range(4):
                    i = qh*4 + il
                    sb = il % 2
                    P.op("tensor", lambda e, s=s, c=c, i=i, sb=sb: e.matmul(psS[:, sb*512:sb*512+128], ko[s][:, c, i*128:(i+1)*128],
                                                                      qo[s][:, c, i*128:(i+1)*128], start=True, stop=False),
                         reads=[("dako", s, c), ("daqo", s, c)], writes=[("psS", sb)])
                    P.op("tensor", lambda e, h=h, sb=sb: e.matmul(psS[:, sb*512:sb*512+128], ident[:], dbm_hi[:, h, :], start=False, stop=False),
                         reads=["ident", "dabmhi"], writes=[("psS", sb)])
                    P.op("tensor", lambda e, h=h, sb=sb: e.matmul(psS[:, sb*512:sb*512+128], ident[:], dbm_lo[:, h, :], start=False, stop=True),
                         reads=["ident", "dabmlo"], writes=[("psS", sb)])
                    pb = pcnt % 4
                    pcnt += 1
                    P.op("scalar", lambda e, pb=pb, sb=sb: e.activation(out=dpT[pb][:, 0:128], in_=psS[:, sb*512:sb*512+128], func=AF.Exp),
                         reads=[("psS", sb)], writes=[("dpT", pb)])
                    last = (il == 3)
                    for a in range(2):
                        P.op("tensor", lambda e, a=a, s=s, i=i, il=il, pb=pb, last=last: e.matmul(psO[a][:, il*128:(il+1)*128], vo[s][:, i, a*128:(a+1)*128],
                                                                                          dpT[pb][:, 0:128], start=False, stop=last),
                             reads=[("davo", s), ("dpT", pb)], writes=[("psO", a)])
                    P.op("tensor", lambda e, il=il, pb=pb, last=last: e.matmul(psD[:, il*128:(il+1)*128], ones_bf[:], dpT[pb][:, 0:128], start=False, stop=last),
                         reads=["ones", ("dpT", pb)], writes=["psD"])
                P.op("vector", lambda e: e.reciprocal(out=rdd[:], in_=psD[:]), reads=["psD"], writes=["rdd"])
                for a in range(2):
                    if c == 0:
                        P.op("vector", lambda e, a=a: e.tensor_tensor(out=o0[a][:], in0=psO[a][:], in1=rdd[:], op=ALU.mult),
                             reads=[("psO", a), "rdd"], writes=[("o0", a)])
                    else:
                        P.op("vector", lambda e, a=a: e.tensor_tensor(out=tt_[:], in0=psO[a][:], in1=rdd[:], op=ALU.mult),
                             reads=[("psO", a), "rdd"], writes=["tt"])
                        P.op("vector", lambda e, a=a: e.scalar_tensor_tensor(out=yy[a][:], in0=tt_[:], scalar=lams[:, 3:4], in1=o0[a][:],
                                                                             op0=ALU.mult, op1=ALU.add),
                             reads=["tt", ("o0", a), "nlam"], writes=[("yy", a)])
            for a in range(2):
                P.op("scalar", lambda e, a=a: e.activation(out=sqd[a][:], in_=yy[a][:], func=AF.Square),
                     reads=[("yy", a)], writes=[("sqd", a)])
                P.op("tensor", lambda e, a=a: e.matmul(psN[:], ones_bf[:], sqd[a][:], start=(a == 0), stop=(a == 1)),
                     reads=["ones", ("sqd", a)], writes=["psN"])
            P.op("scalar", lambda e: e.activation(out=rsd[:], in_=psN[:], func=AF.Sqrt, bias=EPS, scale=1.0/256),
                 reads=["psN"], writes=["rsd"])
            P.op("vector", lambda e: e.reciprocal(out=rsd[:], in_=rsd[:]), reads=["rsd"], writes=["rsd"])
            for a in range(2):
                yb = ycnt % 4
                ycnt += 1
                P.op("vector", lambda e, a=a, yb=yb: e.scalar_tensor_tensor(out=ycs[yb][:], in0=yy[a][:], scalar=gsub[:, a:a+1], in1=rsd[:],
                                                                     op0=ALU.mult, op1=ALU.mult),
                     reads=[("yy", a), "rsd", "gsub"], writes=[("ycs", yb)])
                r0 = h*256 + a*128
                outs.append(P.dma("sync", o["ycT"][r0:r0+128, qh*512:(qh+1)*512], ycs[yb][:], key="ycs%d" % yb, reads=[("ycs", yb)]))
    return outs


S = 4096
NB = 8
BL = 512


def emit_ssm(nc, P, st, i_, o):
    PI = math.pi
    prm = st.enter_context(nc.sbuf_tensor("t_prm", [128, 3, 8], F32))
    w = {n: st.enter_context(nc.sbuf_tensor("t_" + n, [128, 8], F32)) for n in
         ("re", "dt", "rdt", "th", "r", "cs", "sn", "t0", "t1", "nr", "ni", "den", "fre", "fim", "nfim", "thr")}
    iota = st.enter_context(nc.sbuf_tensor("t_iota", [128, 513], F32))
    ang = st.enter_context(nc.sbuf_tensor("t_ang", [128, 513], F32))
    ang2 = st.enter_context(nc.sbuf_tensor("t_ang2", [128, 513], F32))
    ang3 = st.enter_context(nc.sbuf_tensor("t_ang3", [128, 513], F32))
    cosT = st.enter_context(nc.sbuf_tensor("t_cosT", [128, 8, 513], F32))
    sinT = st.enter_context(nc.sbuf_tensor("t_sinT", [128, 8, 513], F32))
    rr = st.enter_context(nc.sbuf_tensor("t_rr", [128, 8, BL], F32))
    onesf = st.enter_context(nc.sbuf_tensor("t_onesf", [128, BL], F32))
    bt = st.enter_context(nc.sbuf_tensor("t_bt", [128, 8, 2, 128], BF16))
    ctf = st.enter_context(nc.sbuf_tensor("t_ctf", [128, 8, 2, 128], F32))
    ctmp = st.enter_context(nc.sbuf_tensor("t_ctmp", [128, 2, 128], F32))
    cp = st.enter_context(nc.sbuf_tensor("t_cp", [128, 8, 2, 128], BF16))
    ub = st.enter_context(nc.sbuf_tensor("t_ub", [128, 2, S], BF16))
    uf = st.enter_context(nc.sbuf_tensor("t_uf", [128, 2, S], F32))
    dsk = st.enter_context(nc.sbuf_tensor("t_dsk", [128, 2], F32))
    carry = st.enter_context(nc.sbuf_tensor("t_carry", [128, 8, 4], F32))
    NW = 2
    bus = [[st.enter_context(nc.sbuf_tensor("t_bu%d_%d" % (k, b), [128, BL], F32)) for b in range(NW)] for k in range(2)]
    tm = [[st.enter_context(nc.sbuf_tensor("t_tm%d_%d" % (k, b), [128, BL], F32)) for b in range(NW)] for k in range(4)]
    ww = [[st.enter_context(nc.sbuf_tensor("t_w%d_%d" % (k, b), [128, BL], F32)) for b in range(NW)] for k in range(2)]
    zz = [[st.enter_context(nc.sbuf_tensor("t_z%d_%d" % (k, b), [128, BL], F32)) for b in range(NW)] for k in range(2)]
    xx = [[st.enter_context(nc.sbuf_tensor("t_x%d_%d" % (k, b), [128, BL], BF16)) for b in range(NW)] for k in range(2)]
    yv = [st.enter_context(nc.sbuf_tensor("t_yv%d" % b, [128, BL], F32)) for b in range(2)]
    g1 = [st.enter_context(nc.sbuf_tensor("t_g1%d" % b, [128, BL], F32)) for b in range(2)]
    zo = [st.enter_context(nc.sbuf_tensor("t_zo%d" % b, [128, BL], BF16)) for b in range(2)]
    psB = [st.enter_context(nc.psum_tensor("t_psB%d" % k, [128, BL], F32)) for k in range(4)]
    psY = [st.enter_context(nc.psum_tensor("t_psY%d" % k, [128, BL], F32)) for k in range(2)]
    outs = []
    V = "vector"
    G = "gpsimd"

    P.dma("sync", prm[:], i_["s_prm"], key="sprm", writes=["prm"])
    P.dma("sync", iota[:], i_["s_iota"], key="siota", writes=["iota"])
    P.dma("sync", dsk[:], i_["s_dsk"], key="sdsk", writes=["dsk"])
    P.dma("gpsimd", bt[:], i_["s_bt"], key="sbt", writes=["bt"])
    P.dma("sync", ctf[:], i_["s_ct"], key="sct", writes=["ctf"])
    uv = i_["s_uT"].rearrange("(c p) t -> p c t", p=128)
    for c in range(2):
        for q in range(2):
            P.dma("sync", uf[:, c, q*2048:(q+1)*2048], uv[:, c, q*2048:(q+1)*2048], key="suf", writes=[("uf", c, q)])
            P.dma("gpsimd", ub[:, c, q*2048:(q+1)*2048], uv[:, c, q*2048:(q+1)*2048], key="sub", writes=[("ub", c, q)])
    P.op(V, lambda e: e.memset(onesf[:], 1.0), writes=["onesf"])
    P.op(V, lambda e: e.memset(carry[:], 0.0), writes=["carry"])

    def vop(name_out, fn, reads):
        P.op(V, fn, reads=reads, writes=[("p", name_out)])

    a_re = prm[:, 0, :]
    a_im = prm[:, 1, :]
    ldt = prm[:, 2, :]
    vop("re", lambda e: e.tensor_scalar_min(out=w["re"][:], in0=a_re, scalar1=-1e-4), ["prm"])
    P.op("scalar", lambda e: e.activation(out=w["dt"][:], in_=ldt, func=AF.Exp), reads=["prm"], writes=[("p", "dt")])
    vop("rdt", lambda e: e.tensor_tensor(out=w["rdt"][:], in0=w["re"][:], in1=w["dt"][:], op=ALU.mult), [("p", "re"), ("p", "dt")])
    vop("th", lambda e: e.tensor_tensor(out=w["th"][:], in0=a_im, in1=w["dt"][:], op=ALU.mult), ["prm", ("p", "dt")])
    P.op("scalar", lambda e: e.activation(out=w["r"][:], in_=w["rdt"][:], func=AF.Exp), reads=[("p", "rdt")], writes=[("p", "r")])
    MAGIC = 12582912.0
    TWO_PI_HI = 6.2831854820251465
    TWO_PI_LO = -1.7484556000744883e-07

    def reduce_angle(dst, src, shift, tmp, reads, wkey):
        P.op(V, lambda e: e.tensor_scalar(out=tmp, in0=src, scalar1=shift, scalar2=1.0 / (2 * PI), op0=ALU.add, op1=ALU.mult), reads=reads, writes=["ra_t"])
        P.op(V, lambda e: e.tensor_scalar_add(out=tmp, in0=tmp, scalar1=MAGIC), reads=["ra_t"], writes=["ra_t"])
        P.op(V, lambda e: e.tensor_scalar_add(out=tmp, in0=tmp, scalar1=-MAGIC), reads=["ra_t"], writes=["ra_t"])
        P.op(V, lambda e: e.tensor_scalar(out=dst, in0=tmp, scalar1=-TWO_PI_HI, scalar2=shift, op0=ALU.mult, op1=ALU.add), reads=["ra_t"] + reads, writes=[wkey, "ra_d"])
        P.op(V, lambda e: e.tensor_tensor(out=dst, in0=dst, in1=src, op=ALU.add), reads=["ra_d"] + reads, writes=[wkey, "ra_d"])
        P.op(V, lambda e: e.scalar_tensor_tensor(out=dst, in0=tmp, scalar=-TWO_PI_LO, in1=dst, op0=ALU.mult, op1=ALU.add), reads=["ra_t", "ra_d"], writes=[wkey, "ra_d"])

    reduce_angle(w["thr"][:], w["th"][:], 0.0, w["t0"][:], [("p", "th")], ("p", "thr"))

    def sincos(out_sin, out_cos, src_ap, big, reads, wkeys):
        tmp = ang2[:] if big else w["t1"][:]
        red = ang[:] if big else w["t0"][:]
        reduce_angle(red, src_ap, 0.0, tmp, reads, "sc_r")
        P.op("scalar", lambda e: e.activation(out=out_sin, in_=red, func=AF.Sin), reads=["sc_r"], writes=[wkeys[0]])
        reduce_angle(red, src_ap, 0.5 * PI, tmp, reads + [wkeys[0]], "sc_r")
        P.op("scalar", lambda e: e.activation(out=out_cos, in_=red, func=AF.Sin), reads=["sc_r"], writes=[wkeys[1]])

    sincos(w["sn"][:], w["cs"][:], w["thr"][:], False, [("p", "thr")], [("p", "sn"), ("p", "cs")])
    vop("nr", lambda e: e.tensor_tensor(out=w["nr"][:], in0=w["r"][:], in1=w["cs"][:], op=ALU.mult), [("p", "r"), ("p", "cs")])
    vop("nr", lambda e: e.tensor_scalar_add(out=w["nr"][:], in0=w["nr"][:], scalar1=-1.0), [("p", "nr")])
    vop("ni", lambda e: e.tensor_tensor(out=w["ni"][:], in0=w["r"][:], in1=w["sn"][:], op=ALU.mult), [("p", "r"), ("p", "sn")])
    vop("den", lambda e: e.tensor_tensor(out=w["den"][:], in0=w["re"][:], in1=w["re"][:], op=ALU.mult), [("p", "re")])
    vop("t1", lambda e: e.tensor_tensor(out=w["t1"][:], in0=a_im, in1=a_im, op=ALU.mult), ["prm"])
    vop("den", lambda e: e.tensor_tensor(out=w["den"][:], in0=w["den"][:], in1=w["t1"][:], op=ALU.add), [("p", "den"), ("p", "t1")])
    vop("den", lambda e: e.reciprocal(out=w["den"][:], in_=w["den"][:]), [("p", "den")])
    vop("fre", lambda e: e.tensor_tensor(out=w["fre"][:], in0=w["nr"][:], in1=w["re"][:], op=ALU.mult), [("p", "nr"), ("p", "re")])
    vop("t1", lambda e: e.tensor_tensor(out=w["t1"][:], in0=w["ni"][:], in1=a_im, op=ALU.mult), [("p", "ni"), "prm"])
    vop("fre", lambda e: e.tensor_tensor(out=w["fre"][:], in0=w["fre"][:], in1=w["t1"][:], op=ALU.add), [("p", "fre"), ("p", "t1")])
    vop("fre", lambda e: e.tensor_tensor(out=w["fre"][:], in0=w["fre"][:], in1=w["den"][:], op=ALU.mult), [("p", "fre"), ("p", "den")])
    vop("fim", lambda e: e.tensor_tensor(out=w["fim"][:], in0=w["ni"][:], in1=w["re"][:], op=ALU.mult), [("p", "ni"), ("p", "re")])
    vop("t1", lambda e: e.tensor_tensor(out=w["t1"][:], in0=w["nr"][:], in1=a_im, op=ALU.mult), [("p", "nr"), "prm", ("p", "fre")])
    vop("fim", lambda e: e.tensor_tensor(out=w["fim"][:], in0=w["fim"][:], in1=w["t1"][:], op=ALU.subtract), [("p", "fim"), ("p", "t1")])
    vop("fim", lambda e: e.tensor_tensor(out=w["fim"][:], in0=w["fim"][:], in1=w["den"][:], op=ALU.mult), [("p", "fim"), ("p", "den")])
    vop("nfim", lambda e: e.tensor_scalar_mul(out=w["nfim"][:], in0=w["fim"][:], scalar1=-1.0), [("p", "fim")])

    for gp in range(8):
        P.op(V, lambda e, gp=gp: e.tensor_scalar_mul(out=ang3[:], in0=iota[:], scalar1=w["thr"][:, gp:gp+1]), reads=["iota", ("p", "thr"), ("sinT", gp - 1), ("cosT", gp - 1)], writes=["ang3"])
        sincos(sinT[:, gp, :], cosT[:, gp, :], ang3[:], True, ["ang3"], [("sinT", gp), ("cosT", gp)])
        P.op(V, lambda e, gp=gp: e.tensor_scalar_mul(out=rr[:, gp, :], in0=onesf[:], scalar1=w["r"][:, gp:gp+1]), reads=["onesf", ("p", "r")], writes=[("rr", gp)])
        P.op(V, lambda e, gp=gp: e.tensor_scalar_mul(out=ctmp[:, 0, :], in0=ctf[:, gp, 1, :], scalar1=w["nfim"][:, gp:gp+1]), reads=["ctf", ("p", "nfim")], writes=["ctmp0"])
        P.op(V, lambda e, gp=gp: e.scalar_tensor_tensor(out=cp[:, gp, 0, :], in0=ctf[:, gp, 0, :], scalar=w["fre"][:, gp:gp+1], in1=ctmp[:, 0, :], op0=ALU.mult, op1=ALU.add),
             reads=["ctf", ("p", "fre"), "ctmp0"], writes=[("cp", gp)])
        P.op(V, lambda e, gp=gp: e.tensor_scalar_mul(out=ctmp[:, 1, :], in0=ctf[:, gp, 1, :], scalar1=w["fre"][:, gp:gp+1]), reads=["ctf", ("p", "fre")], writes=["ctmp1"])
        P.op(V, lambda e, gp=gp: e.scalar_tensor_tensor(out=ctmp[:, 1, :], in0=ctf[:, gp, 0, :], scalar=w["fim"][:, gp:gp+1], in1=ctmp[:, 1, :], op0=ALU.mult, op1=ALU.add),
             reads=["ctf", ("p", "fim"), "ctmp1"], writes=["ctmp1"])
        P.op(V, lambda e, gp=gp: e.tensor_scalar_mul(out=cp[:, gp, 1, :], in0=ctmp[:, 1, :], scalar1=-1.0), reads=["ctmp1", ("cp", gp)], writes=[("cp", gp)])

    cnt = 0
    ycnt = 0
    for blk in range(NB):
        t0_ = blk * BL
        for ct in range(2):
            for gl in range(4):
                gp = ct * 4 + gl
                b = cnt % NW
                pb = (cnt % 2) * 2
                cnt += 1
                cs = cosT[:, gp, 0:BL]
                sn = sinT[:, gp, 0:BL]
                for k in range(2):
                    P.op("tensor", lambda e, k=k, gp=gp, ct=ct, pb=pb, t0_=t0_: e.matmul(psB[pb + k][:], bt[:, gp, k, :], ub[:, ct, t0_:t0_+BL], start=True, stop=True),
                         reads=["bt", ("ub", ct, t0_ // 2048)], writes=[("psB", pb + k)])
                    P.op("scalar", lambda e, k=k, b=b, pb=pb: e.copy(out=bus[k][b][:], in_=psB[pb + k][:]), reads=[("psB", pb + k)], writes=[("bu", k, b)])
                P.op(V, lambda e, b=b, cs=cs: e.tensor_tensor(out=tm[0][b][:], in0=bus[0][b][:], in1=cs, op=ALU.mult), reads=[("bu", 0, b), ("cosT", gp)], writes=[("tm", 0, b)])
                P.op(G, lambda e, b=b, sn=sn: e.tensor_tensor(out=tm[1][b][:], in0=bus[1][b][:], in1=sn, op=ALU.mult), reads=[("bu", 1, b), ("sinT", gp)], writes=[("tm", 1, b)])
                P.op(V, lambda e, b=b: e.tensor_tensor(out=ww[0][b][:], in0=tm[0][b][:], in1=tm[1][b][:], op=ALU.add), reads=[("tm", 0, b), ("tm", 1, b)], writes=[("w", 0, b)])
                P.op(G, lambda e, b=b, cs=cs: e.tensor_tensor(out=tm[2][b][:], in0=bus[1][b][:], in1=cs, op=ALU.mult), reads=[("bu", 1, b), ("cosT", gp)], writes=[("tm", 2, b)])
                P.op(G, lambda e, b=b, sn=sn: e.tensor_tensor(out=tm[3][b][:], in0=bus[0][b][:], in1=sn, op=ALU.mult), reads=[("bu", 0, b), ("sinT", gp)], writes=[("tm", 3, b)])
                P.op(G, lambda e, b=b: e.tensor_tensor(out=ww[1][b][:], in0=tm[2][b][:], in1=tm[3][b][:], op=ALU.subtract), reads=[("tm", 2, b), ("tm", 3, b)], writes=[("w", 1, b)])
                for k in range(2):
                    if blk == 0:
                        init = 0.0
                    else:
                        init = carry[:, gp, 2 + k:3 + k]
                    P.op(V, lambda e, k=k, b=b, gp=gp, init=init: e.tensor_tensor_scan(out=zz[k][b][:], data0=rr[:, gp, :], data1=ww[k][b][:], initial=init,
                                                                                op0=ALU.mult, op1=ALU.add),
                         reads=[("rr", gp), ("w", k, b), ("carry", gp)], writes=[("z", k, b)])
                if blk < NB - 1:
                    c512 = cosT[:, gp, BL:BL+1]
                    s512 = sinT[:, gp, BL:BL+1]
                    P.op(V, lambda e, b=b, gp=gp, s512=s512: e.tensor_scalar_mul(out=carry[:, gp, 0:1], in0=zz[1][b][:, BL-1:BL], scalar1=s512),
                         reads=[("z", 1, b), ("sinT", gp), ("carry", gp)], writes=[("carry", gp)])
                    P.op(V, lambda e, b=b, gp=gp, s512=s512: e.tensor_scalar_mul(out=carry[:, gp, 1:2], in0=zz[0][b][:, BL-1:BL], scalar1=s512),
                         reads=[("z", 0, b), ("sinT", gp), ("carry", gp)], writes=[("carry", gp)])
                    P.op(V, lambda e, b=b, gp=gp, c512=c512: e.scalar_tensor_tensor(out=carry[:, gp, 2:3], in0=zz[0][b][:, BL-1:BL], scalar=c512, in1=carry[:, gp, 0:1],
                                                                              op0=ALU.mult, op1=ALU.subtract),
                         reads=[("z", 0, b), ("cosT", gp), ("carry", gp)], writes=[("carry", gp)])
                    P.op(V, lambda e, b=b, gp=gp, c512=c512: e.scalar_tensor_tensor(out=carry[:, gp, 3:4], in0=zz[1][b][:, BL-1:BL], scalar=c512, in1=carry[:, gp, 1:2],
                                                                              op0=ALU.mult, op1=ALU.add),
                         reads=[("z", 1, b), ("cosT", gp), ("carry", gp)], writes=[("carry", gp)])
                P.op(G, lambda e, b=b, cs=cs: e.tensor_tensor(out=tm[0][b][:], in0=zz[0][b][:], in1=cs, op=ALU.mult), reads=[("z", 0, b), ("cosT", gp)], writes=[("tm", 0, b)])
                P.op(V, lambda e, b=b, sn=sn: e.tensor_tensor(out=tm[1][b][:], in0=zz[1][b][:], in1=sn, op=ALU.mult), reads=[("z", 1, b), ("sinT", gp)], writes=[("tm", 1, b)])
                P.op(G, lambda e, b=b: e.tensor_tensor(out=xx[0][b][:], in0=tm[0][b][:], in1=tm[1][b][:], op=ALU.subtract), reads=[("tm", 0, b), ("tm", 1, b)], writes=[("x", 0, b)])
                P.op(G, lambda e, b=b, sn=sn: e.tensor_tensor(out=tm[2][b][:], in0=zz[0][b][:], in1=sn, op=ALU.mult), reads=[("z", 0, b), ("sinT", gp)], writes=[("tm", 2, b)])
                P.op(V, lambda e, b=b, cs=cs: e.tensor_tensor(out=tm[3][b][:], in0=zz[1][b][:], in1=cs, op=ALU.mult), reads=[("z", 1, b), ("cosT", gp)], writes=[("tm", 3, b)])
                P.op(G, lambda e, b=b: e.tensor_tensor(out=xx[1][b][:], in0=tm[2][b][:], in1=tm[3][b][:], op=ALU.add), reads=[("tm", 2, b), ("tm", 3, b)], writes=[("x", 1, b)])
                for k in range(2):
                    P.op("tensor", lambda e, k=k, b=b, gp=gp, ct=ct, gl=gl: e.matmul(psY[ct][:], cp[:, gp, k, :], xx[k][b][:], start=(gl == 0 and k == 0), stop=(gl == 3 and k == 1)),
                         reads=[("cp", gp), ("x", k, b)], writes=[("psY", ct)])
            yb_ = ycnt % 2
            ycnt += 1
            P.op(V, lambda e, ct=ct, yb_=yb_, t0_=t0_: e.scalar_tensor_tensor(out=yv[yb_][:], in0=uf[:, ct, t0_:t0_+BL], scalar=dsk[:, ct:ct+1], in1=psY[ct][:], op0=ALU.mult, op1=ALU.add),
                 reads=[("uf", ct, t0_ // 2048), "dsk", ("psY", ct)], writes=[("yv", yb_)])
            P.op(G, lambda e, yb_=yb_: e.tensor_tensor(out=g1[yb_][:], in0=yv[yb_][:], in1=yv[yb_][:], op=ALU.mult), reads=[("yv", yb_)], writes=[("g1", yb_)])
            P.op(G, lambda e, yb_=yb_: e.tensor_scalar(out=g1[yb_][:], in0=g1[yb_][:], scalar1=0.044715, scalar2=1.0, op0=ALU.mult, op1=ALU.add), reads=[("g1", yb_)], writes=[("g1", yb_)])
            P.op(G, lambda e, yb_=yb_: e.tensor_tensor(out=g1[yb_][:], in0=g1[yb_][:], in1=yv[yb_][:], op=ALU.mult), reads=[("g1", yb_), ("yv", yb_)], writes=[("g1", yb_)])
            P.op("scalar", lambda e, yb_=yb_: e.activation(out=g1[yb_][:], in_=g1[yb_][:], func=AF.Sigmoid, scale=1.5957691216057308), reads=[("g1", yb_)], writes=[("g1", yb_)])
            P.op(V, lambda e, yb_=yb_: e.tensor_tensor(out=zo[yb_][:], in0=g1[yb_][:], in1=yv[yb_][:], op=ALU.mult), reads=[("g1", yb_), ("yv", yb_)], writes=[("zo", yb_)])
            outs.append(P.dma("sync", o["zT"][ct*128:(ct+1)*128, t0_:t0_+BL], zo[yb_][:], key="szo%d" % yb_, reads=[("zo", yb_)]))
    return outs


T = 1024


def emit_cmix(nc, P, st, i_, o):
    ybuf = st.enter_context(nc.sbuf_tensor("c_ybuf", [128, 32, T], BF16))
    zbuf = st.enter_context(nc.sbuf_tensor("c_zbuf", [128, 8, T], BF16))
    mrg = st.enter_context(nc.sbuf_tensor("c_mrg", [128, 32, T], BF16))
    wsl = [st.enter_context(nc.sbuf_tensor("c_w%d" % s, [128, 8192], BF16)) for s in range(2)]
    bgl = st.enter_context(nc.sbuf_tensor("c_bgl", [128, 8], F32))
    gts = [[st.enter_context(nc.sbuf_tensor("c_g%d_%d" % (k, s), [128, 512], BF16)) for s in range(2)] for k in range(3)]
    xt = [st.enter_context(nc.sbuf_tensor("c_xt%d" % s, [128, 512], F32)) for s in range(2)]
    tf = [st.enter_context(nc.sbuf_tensor("c_tf%d" % s, [128, 512], F32)) for s in range(3)]
    gl = [st.enter_context(nc.sbuf_tensor("c_gl%d" % s, [128, 512], BF16)) for s in range(2)]
    ps = [st.enter_context(nc.psum_tensor("c_ps%d" % s, [128, 512], F32)) for s in range(6)]
    outs = []
    V = "vector"
    P.dma("sync", bgl[:], i_["b_glu"], key="cbgl", writes=["bgl"])
    zv = i_["zT"].rearrange("(k p) t -> p k t", p=128)
    P.dma("sync", zbuf[:], zv, key="czb", writes=["zbuf"])
    ybv = i_["ybT"].rearrange("(k p) t -> p k t", p=128)
    ycv = i_["ycT"].rearrange("(k p) t -> p k t", p=128)
    for q in range(3):
        P.dma("sync", ybuf[:, 8 + q*4:12 + q*4, :], ybv[:, q*4:(q+1)*4, :], key="cyb", writes=[("ybuf", 2 + q)])
        P.dma("sync", ybuf[:, 20 + q*4:24 + q*4, :], ycv[:, q*4:(q+1)*4, :], key="cyc", writes=[("ybuf", 5 + q)])
    wcnt = [0]

    def load_w(src, nk, c0, ncols):
        s = wcnt[0] % 2
        wcnt[0] += 1
        view = wsl[s][:, 0:nk*ncols].rearrange("p (k c) -> p k c", c=ncols)
        sv = src.rearrange("(k p) c -> p k c", p=128)
        P.dma("gpsimd", view, sv[:, :, c0:c0+ncols], key="cw%d" % s, writes=[("cw", s)])
        return s, view

    pcnt = [0]

    def nb():
        b = pcnt[0] % 6
        pcnt[0] += 1
        return b

    for pn in range(2):
        s, wv_ = load_w(i_["w_glu"], 8, pn*512, 512)
        for m in range(4):
            mt = pn*4 + m
            for hf in range(2):
                b = nb()
                for kt in range(8):
                    P.op("tensor", lambda e, b=b, wv_=wv_, kt=kt, m=m, hf=hf: e.matmul(ps[b][:], wv_[:, kt, m*128:(m+1)*128], zbuf[:, kt, hf*512:(hf+1)*512], start=(kt == 0), stop=(kt == 7)),
                         reads=[("cw", s), "zbuf"], writes=[("cps", b)])
                g = (mt*2 + hf) % 2
                P.op("scalar", lambda e, b=b, g=g, mt=mt: e.activation(out=gl[g][:], in_=ps[b][:], func=AF.Sigmoid, bias=bgl[:, mt:mt+1]),
                     reads=[("cps", b), "bgl"], writes=[("gl", g)])
                P.op(V, lambda e, g=g, mt=mt, hf=hf: e.tensor_tensor(out=ybuf[:, mt, hf*512:(hf+1)*512], in0=gl[g][:], in1=zbuf[:, mt, hf*512:(hf+1)*512], op=ALU.mult),
                     reads=[("gl", g), "zbuf"], writes=[("ybuf", mt // 4)])
    srcs = [(i_["w_out_a"], 8, 0), (i_["w_out_b"], 12, 8), (i_["w_out_c"], 12, 20)]
    gcnt = 0
    for pn in range(8):
        wvs = [load_w(src, nk, pn*512, 512) for (src, nk, k0) in srcs[:2]]
        for m in range(4):
            mt = pn*4 + m
            for hf in range(2):
                gs = gcnt % 2
                gcnt += 1
                for k in range(3):
                    r0 = k*4096 + mt*128
                    P.dma("sync", gts[k][gs][:], i_["gT"][r0:r0+128, hf*512:(hf+1)*512], key="cg%d_%d" % (k, gs), writes=[("gts", k, gs)])
                bs = []
                for k in range(2):
                    src, nk, k0 = srcs[k]
                    s, wv_ = wvs[k]
                    b = nb()
                    bs.append(b)
                    for kt in range(nk):
                        P.op("tensor", lambda e, b=b, wv_=wv_, kt=kt, m=m, hf=hf, k0=k0, nk=nk: e.matmul(ps[b][:], wv_[:, kt, m*128:(m+1)*128], ybuf[:, k0 + kt, hf*512:(hf+1)*512],
                                                                                          start=(kt == 0), stop=(kt == nk-1)),
                             reads=[("cw", s), ("ybuf", (k0 + kt) // 4)], writes=[("cps", b)])
                pend = (mt, hf, gs, bs)
                P.op(V, lambda e, gs=gs, b=bs[0]: e.tensor_tensor(out=tf[0][:], in0=ps[b][:], in1=gts[0][gs][:], op=ALU.mult), reads=[("cps", bs[0]), ("gts", 0, gs)], writes=[("tf", 0)])
                P.op(V, lambda e, gs=gs, b=bs[1]: e.tensor_tensor(out=tf[1][:], in0=ps[b][:], in1=gts[1][gs][:], op=ALU.mult), reads=[("cps", bs[1]), ("gts", 1, gs)], writes=[("tf", 1)])
                P.op(V, lambda e, mt=mt, hf=hf: e.tensor_tensor(out=mrg[:, mt, hf*512:(hf+1)*512], in0=tf[0][:], in1=tf[1][:], op=ALU.add), reads=[("tf", 0), ("tf", 1)], writes=[("mrg", mt, hf)])
        s, wv_ = load_w(srcs[2][0], 12, pn*512, 512)
        for m in range(4):
            mt = pn*4 + m
            for hf in range(2):
                gs = gcnt % 2
                gcnt += 1
                r0 = 2*4096 + mt*128
                P.dma("sync", gts[2][gs][:], i_["gT"][r0:r0+128, hf*512:(hf+1)*512], key="cg2_%d" % gs, writes=[("gts", 2, gs)])
                b = nb()
                for kt in range(12):
                    P.op("tensor", lambda e, b=b, wv_=wv_, kt=kt, m=m, hf=hf: e.matmul(ps[b][:], wv_[:, kt, m*128:(m+1)*128], ybuf[:, 20 + kt, hf*512:(hf+1)*512], start=(kt == 0), stop=(kt == 11)),
                         reads=[("cw", s), ("ybuf", (20 + kt) // 4)], writes=[("cps", b)])
                P.op(V, lambda e, gs=gs, b=b: e.tensor_tensor(out=tf[2][:], in0=ps[b][:], in1=gts[2][gs][:], op=ALU.mult), reads=[("cps", b), ("gts", 2, gs)], writes=[("tf", 2)])
                P.op(V, lambda e, mt=mt, hf=hf: e.tensor_tensor(out=mrg[:, mt, hf*512:(hf+1)*512], in0=mrg[:, mt, hf*512:(hf+1)*512], in1=tf[2][:], op=ALU.add),
                     reads=[("tf", 2), ("mrg", mt, hf)], writes=[("mrg", mt, hf)])
    xcnt = 0
    for pn in range(16):
        s, wv_ = load_w(i_["w_o"], 32, pn*256, 256)
        for m in range(2):
            mt = pn*2 + m
            for hf in range(2):
                xs = xcnt % 2
                xcnt += 1
                P.dma("sync", xt[xs][:], i_["xT"][mt*128:(mt+1)*128, hf*512:(hf+1)*512], key="cxt%d" % xs, writes=[("xt", xs)])
                b = nb()
                for kt in range(32):
                    P.op("tensor", lambda e, b=b, wv_=wv_, kt=kt, m=m, hf=hf: e.matmul(ps[b][:], wv_[:, kt, m*128:(m+1)*128], mrg[:, kt, hf*512:(hf+1)*512], start=(kt == 0), stop=(kt == 31)),
                         reads=[("cw", s), ("mrg", kt, hf)], writes=[("cps", b)])
                P.op(V, lambda e, b=b, xs=xs: e.tensor_tensor(out=xt[xs][:], in0=ps[b][:], in1=xt[xs][:], op=ALU.add), reads=[("cps", b), ("xt", xs)], writes=[("xt", xs)])
                outs.append(P.dma("sync", o["x1T"][mt*128:(mt+1)*128, hf*512:(hf+1)*512], xt[xs][:], key="cxo%d" % xs, reads=[("xt", xs)]))
    return outs


def emit_cmlp(nc, P, st, i_, o):
    hT = st.enter_context(nc.sbuf_tensor("m_hT", [128, 32, T], BF16))
    hid = st.enter_context(nc.sbuf_tensor("m_hid", [128, 16, T], BF16))
    ones_bf = st.enter_context(nc.sbuf_tensor("m_ones", [128, 128], BF16))
    gain_sb = st.enter_context(nc.sbuf_tensor("m_gain", [128, 32], F32))
    wsl = [st.enter_context(nc.sbuf_tensor("m_w%d" % s, [128, 8192], BF16)) for s in range(2)]
    sqt = [st.enter_context(nc.sbuf_tensor("m_sq%d" % s, [128, 512], BF16)) for s in range(2)]
    acc = [st.enter_context(nc.sbuf_tensor("m_acc%d" % s, [128, T], F32)) for s in range(3)]
    ps = [st.enter_context(nc.psum_tensor("m_ps%d" % s, [128, 512], F32)) for s in range(8)]
    outs = []
    V = "vector"
    P.op(V, lambda e: e.memset(ones_bf[:], 1.0), writes=["ones"])
    P.dma("sync", gain_sb[:], i_["gain2"], key="mgain", writes=["gain"])
    emit_rmsnorm(nc, P, st, i_["x1T"], gain_sb, hT, ones_bf, (ps[6], ps[7]), "n2")
    wcnt = [0]

    def load_w(src_rows, nk, c0, ncols):
        s = wcnt[0] % 2
        wcnt[0] += 1
        view = wsl[s][:, 0:nk*ncols].rearrange("p (k c) -> p k c", c=ncols)
        sv = src_rows.rearrange("(k p) c -> p k c", p=128)
        P.dma("gpsimd", view, sv[:, :, c0:c0+ncols], key="mw%d" % s, writes=[("mw", s)])
        return s, view

    pcnt = [0]

    def nb():
        b = pcnt[0] % 6
        pcnt[0] += 1
        return b
    scnt = 0
    acnt = 0
    for fs in range(8):
        for pn in range(8):
            s, wv_ = load_w(i_["w_ff1"], 32, fs*2048 + pn*256, 256)
            for m in range(2):
                ft = pn*2 + m
                for hf in range(2):
                    b = nb()
                    for kt in range(32):
                        P.op("tensor", lambda e, b=b, wv_=wv_, kt=kt, m=m, hf=hf: e.matmul(ps[b][:], wv_[:, kt, m*128:(m+1)*128], hT[:, kt, hf*512:(hf+1)*512], start=(kt == 0), stop=(kt == 31)),
                             reads=[("mw", s), ("hT", kt // 8)], writes=[("mps", b)])
                    q = scnt % 2
                    scnt += 1
                    P.op("scalar", lambda e, b=b, q=q: e.activation(out=sqt[q][:], in_=ps[b][:], func=AF.Square), reads=[("mps", b)], writes=[("sqt", q)])
                    P.op(V, lambda e, b=b, q=q, ft=ft, hf=hf: e.scalar_tensor_tensor(out=hid[:, ft, hf*512:(hf+1)*512], in0=ps[b][:], scalar=0.0, in1=sqt[q][:], op0=ALU.is_gt, op1=ALU.mult),
                         reads=[("mps", b), ("sqt", q)], writes=[("hid", ft, hf)])
        w2rows = i_["w_ff2"][fs*2048:(fs+1)*2048, :]
        for pn in range(8):
            s, wv_ = load_w(w2rows, 16, pn*512, 512)
            for m in range(4):
                mt = pn*4 + m
                a = acnt % 3
                acnt += 1
                src = i_["x1T"] if fs == 0 else o["x2T"]
                rd = [("x2", mt)] if fs > 0 else []
                P.dma("sync", acc[a][:], src[mt*128:(mt+1)*128, :], key="macc%d" % a, reads=rd, writes=[("acc", a)])
                for hf in range(2):
                    b = nb()
                    for kt in range(16):
                        P.op("tensor", lambda e, b=b, wv_=wv_, kt=kt, m=m, hf=hf: e.matmul(ps[b][:], wv_[:, kt, m*128:(m+1)*128], hid[:, kt, hf*512:(hf+1)*512], start=(kt == 0), stop=(kt == 15)),
                             reads=[("mw", s), ("hid", kt, hf)], writes=[("mps", b)])
                    P.op(V, lambda e, b=b, a=a, hf=hf: e.tensor_tensor(out=acc[a][:, hf*512:(hf+1)*512], in0=ps[b][:], in1=acc[a][:, hf*512:(hf+1)*512], op=ALU.add),
                         reads=[("mps", b), ("acc", a)], writes=[("acc", a)])
                d = P.dma("sync", o["x2T"][mt*128:(mt+1)*128, :], acc[a][:], key="mout%d" % a, reads=[("acc", a)], writes=[("x2", mt)])
                if fs == 7:
                    outs.append(d)
    return outs

NEG = -30000.0

def alibi_slopes(n):
    return (2.0 ** (-8.0 * np.arange(1, n + 1, dtype=np.float32) / n)).astype(np.float32)

def ca_bm_table(rel_bias):
    kl = np.arange(128)[:, None]
    ql = np.arange(128)[None, :]
    out = np.empty((128, 60, 128), np.float32)
    for d in range(5):
        idx = np.clip(128 * d + ql - kl, -128, 128) + 128
        if d == 0:
            allowed = (kl // 64) <= (ql // 64)
        elif d == 4:
            allowed = (kl // 64) >= (ql // 64)
        else:
            allowed = np.ones((128, 128), bool)
        for h in range(12):
            out[:, h * 5 + d, :] = np.where(allowed, rel_bias[h][idx], np.float32(NEG))
    return out

def ca_vm_table(r):
    vm = np.ones((128, 12, 128), np.float32)
    if r == 0:
        vm[:, :4, :] = 0.0
    return vm

def da_bcol_table(r):
    sl = alibi_slopes(6)
    kl = np.arange(128, dtype=np.float32)[:, None]
    out = np.empty((128, 6, 8, 32), np.float32)
    for i in range(8):
        gi = 8 * r + i
        q0 = 128.0 * gi
        for j in range(32):
            if j < gi:
                for h in range(6):
                    out[:, h, i, j] = (sl[h] * (128.0 * j + kl[:, 0] - q0 - 127.0))
            else:
                out[:, :, i, j] = NEG
    return out.reshape(128, 6 * 8 * 32)

def da_bm_table():
    sl = alibi_slopes(6)
    kl = np.arange(128, dtype=np.float32)[:, None]
    ql = np.arange(128, dtype=np.float32)[None, :]
    allowed = (kl // 64) <= (ql // 64)
    out = np.empty((128, 6, 128), np.float32)
    for h in range(6):
        out[:, h, :] = np.where(allowed, -sl[h] * np.abs(ql - kl) + sl[h] * (ql - 127.0), np.float32(NEG))
    return out

def ssm_host_inputs(gq, a_re, a_im, log_dt, b_re, b_im, c_re, c_im, d_skip):
    g0 = 16 * gq
    prm = np.zeros((128, 3, 8), np.float32)
    bt = np.zeros((128, 8, 2, 128), np.float32)
    ct = np.zeros((128, 8, 2, 128), np.float32)
    for gp in range(8):
        for g2 in range(2):
            g = g0 + 2 * gp + g2
            rows = slice(g2 * 64, g2 * 64 + 64)
            prm[rows, 0, gp] = a_re[g]
            prm[rows, 1, gp] = a_im[g]
            prm[rows, 2, gp] = log_dt[g]
            chrow = (gp % 4) * 32 + g2 * 16
            bt[chrow:chrow + 16, gp, 0, rows] = b_re[g].T
            bt[chrow:chrow + 16, gp, 1, rows] = b_im[g].T
            ct[rows, gp, 0, chrow:chrow + 16] = c_re[g].T
            ct[rows, gp, 1, chrow:chrow + 16] = c_im[g].T
    dsk = np.ascontiguousarray(d_skip[256 * gq:256 * gq + 256].reshape(2, 128).T)
    iota = np.ascontiguousarray(np.broadcast_to(np.arange(513, dtype=np.float32), (128, 513)))
    return dict(s_prm=prm, s_bt=bt, s_ct=ct, s_dsk=dsk, s_iota=iota)

NCORES = 8
_PROGS = {}


def _dram_in(nc, name, shape, dt=F32):
    return nc.dram_tensor(name, list(shape), dt, kind="ExternalInput").ap()


def _dram_out(nc, name, shape, dt=F32):
    return nc.dram_tensor(name, list(shape), dt, kind="ExternalOutput").ap()


def _build(kind, lambda_init=None):
    key = (kind, lambda_init)
    if key in _PROGS:
        return _PROGS[key]
    nc = bass.Bass("TRN2", target_bir_lowering=False)
    TT = 1024
    with contextlib.ExitStack() as st:
        P = Prog(nc)
        if kind == "A":
            xT = _dram_in(nc, "xT", [4096, TT])
            gain = _dram_in(nc, "gain", [128, 32])
            qkg = _dram_in(nc, "qkg", [128, 4])
            w = _dram_in(nc, "w_in", [4096, IN_W])
            o = {"uT": _dram_out(nc, "uT", [1024, TT])}
            for nm in ("qb", "kb", "qc", "kc"):
                o[nm] = _dram_out(nc, nm, [12, 128, TT], BF16)
            for nm in ("vb", "vc"):
                o[nm] = _dram_out(nc, nm, [TT, 1536], BF16)
            o["gT"] = _dram_out(nc, "gT", [12288, TT], BF16)
            outs = emit_phase_a(nc, P, st, xT, gain, w, qkg, o)
        elif kind == "S":
            i_ = dict(s_prm=_dram_in(nc, "s_prm", [128, 3, 8]), s_bt=_dram_in(nc, "s_bt", [128, 8, 2, 128]),
                      s_ct=_dram_in(nc, "s_ct", [128, 8, 2, 128]), s_dsk=_dram_in(nc, "s_dsk", [128, 2]),
                      s_iota=_dram_in(nc, "s_iota", [128, 513]), s_uT=_dram_in(nc, "s_uT", [256, 4096]))
            o = dict(zT=_dram_out(nc, "zT", [256, 4096], BF16))
            outs = emit_ssm(nc, P, st, i_, o)
        elif kind == "B":
            I = lambda n, s, d=F32: _dram_in(nc, n, s, d)
            i_ = dict(ident=I("ident", [128, 128]), ca_bm=I("ca_bm", [128, 60, 128]), ca_vm=I("ca_vm", [128, 12, 128]),
                      vb_ext=I("vb_ext", [1536, 1536], BF16), kb_ext=I("kb_ext", [12, 128, 1536], BF16), qb=I("qb", [12, 128, 1024], BF16),
                      da_bcol=I("da_bcol", [128, 1536]), da_bm=I("da_bm", [128, 6, 128]), da_gsub=I("da_gsub", [128, 2]), da_lam=I("da_lam", [4, 128]),
                      kc_full=I("kc_full", [12, 128, 4096], BF16), kc_own=I("kc_own", [12, 128, 1024], BF16), qc=I("qc", [12, 128, 1024], BF16),
                      vc_full=I("vc_full", [4096, 1536], BF16), vc_own=I("vc_own", [1024, 1536], BF16))
            o = dict(ybT=_dram_out(nc, "ybT", [1536, 1024], BF16), ycT=_dram_out(nc, "ycT", [1536, 1024], BF16))
            outs = emit_attn(nc, P, st, i_, o, lambda_init)
        elif kind == "Cmix":
            I = lambda n, s, d=F32: _dram_in(nc, n, s, d)
            i_ = dict(zT=I("zT", [1024, TT], BF16), ybT=I("ybT", [1536, TT], BF16), ycT=I("ycT", [1536, TT], BF16), gT=I("gT", [12288, TT], BF16),
                      xT=I("xT", [4096, TT]), w_glu=I("w_glu", [1024, 1024]), b_glu=I("b_glu", [128, 8]), w_out_a=I("w_out_a", [1024, 4096]),
                      w_out_b=I("w_out_b", [1536, 4096]), w_out_c=I("w_out_c", [1536, 4096]), w_o=I("w_o", [4096, 4096]))
            o = dict(x1T=_dram_out(nc, "x1T", [4096, TT]))
            outs = emit_cmix(nc, P, st, i_, o)
        elif kind == "Cmlp":
            I = lambda n, s, d=F32: _dram_in(nc, n, s, d)
            i_ = dict(x1T=I("x1T", [4096, TT]), gain2=I("gain2", [128, 32]), w_ff1=I("w_ff1", [4096, 16384]), w_ff2=I("w_ff2", [16384, 4096]))
            o = dict(x2T=_dram_out(nc, "x2T", [4096, TT]))
            outs = emit_cmlp(nc, P, st, i_, o)
        P.finish(final_waits=outs)
    _PROGS[key] = nc
    return nc


def _run(kind, maps, lambda_init=None):
    nc = _build(kind, lambda_init)
    res = run_bass_kernel_spmd(nc, maps, core_ids=list(range(NCORES)))
    return res.results


def kernel(**inp):
    f32 = np.float32
    g = lambda k: np.asarray(inp[k])
    x = g("x").astype(f32, copy=False)
    C = lambda a: np.ascontiguousarray(a)
    xT = [C(x[c // 4, (c % 4) * 1024:(c % 4 + 1) * 1024, :].T) for c in range(NCORES)]
    ident = np.eye(128, dtype=f32)
    for l in range(2):
        lambda_init = 0.8 - 0.6 * math.exp(-0.3 * l)
        gain = C(g("norm_mix")[l].reshape(32, 128).T)
        qkg = C(np.stack([g("ca_q_gain")[l], g("ca_k_gain")[l], g("da_q_gain")[l], g("da_k_gain")[l]], axis=1))
        w_in = C(g("w_in")[l])
        ra = _run("A", [dict(xT=xT[c], gain=gain, qkg=qkg, w_in=w_in) for c in range(NCORES)])
        del w_in
        maps = []
        for c in range(NCORES):
            b, gq = c // 4, c % 4
            m = ssm_host_inputs(gq, g("ssm_a_re")[l], g("ssm_a_im")[l], g("ssm_log_dt")[l], g("ssm_b_re")[l], g("ssm_b_im")[l],
                                g("ssm_c_re")[l], g("ssm_c_im")[l], g("ssm_d")[l])
            m["s_uT"] = C(np.concatenate([np.asarray(ra[4 * b + r]["uT"])[256 * gq:256 * gq + 256, :] for r in range(4)], axis=1))
            maps.append(m)
        rs = _run("S", maps)
        ca_bm = ca_bm_table(g("ca_rel_bias")[l].astype(f32))
        da_bm = da_bm_table()
        da_gsub = C(g("da_subln_gain")[l].reshape(2, 128).T)
        da_lam = C(np.stack([g("da_lam_q1")[l], g("da_lam_k1")[l], g("da_lam_q2")[l], g("da_lam_k2")[l]], axis=0))
        maps = []
        for c in range(NCORES):
            b, r = c // 4, c % 4
            kb_own = np.asarray(ra[c]["kb"])
            vb_own = np.asarray(ra[c]["vb"])
            kext = np.zeros((12, 128, 1536), kb_own.dtype)
            vext = np.zeros((1536, 1536), vb_own.dtype)
            kext[:, :, 512:] = kb_own
            vext[512:] = vb_own
            if r > 0:
                kext[:, :, :512] = np.asarray(ra[c - 1]["kb"])[:, :, 512:]
                vext[:512] = np.asarray(ra[c - 1]["vb"])[512:]
            kc_full = C(np.concatenate([np.asarray(ra[4 * b + rr]["kc"]) for rr in range(4)], axis=2))
            vc_full = C(np.concatenate([np.asarray(ra[4 * b + rr]["vc"]) for rr in range(4)], axis=0))
            maps.append(dict(ident=ident, ca_bm=ca_bm, ca_vm=ca_vm_table(r), vb_ext=vext, kb_ext=kext, qb=np.asarray(ra[c]["qb"]),
                             da_bcol=da_bcol_table(r), da_bm=da_bm, da_gsub=da_gsub, da_lam=da_lam,
                             kc_full=kc_full, kc_own=np.asarray(ra[c]["kc"]), qc=np.asarray(ra[c]["qc"]),
                             vc_full=vc_full, vc_own=np.asarray(ra[c]["vc"])))
        rb = _run("B", maps, lambda_init)
        b_glu = C(g("ssm_b_glu")[l].reshape(8, 128).T)
        wts = dict(w_glu=C(g("ssm_w_glu")[l]), b_glu=b_glu, w_out_a=C(g("w_out_a")[l]), w_out_b=C(g("w_out_b")[l]),
                   w_out_c=C(g("w_out_c")[l]), w_o=C(g("w_o")[l]))
        maps = []
        for c in range(NCORES):
            b, r = c // 4, c % 4
            zT = C(np.concatenate([np.asarray(rs[4 * b + gq]["zT"])[:, r * 1024:(r + 1) * 1024] for gq in range(4)], axis=0))
            m = dict(zT=zT, ybT=np.asarray(rb[c]["ybT"]), ycT=np.asarray(rb[c]["ycT"]), gT=np.asarray(ra[c]["gT"]), xT=xT[c])
            m.update(wts)
            maps.append(m)
        rc = _run("Cmix", maps)
        del ra, rs, rb, wts
        gain2 = C(g("norm_mlp")[l].reshape(32, 128).T)
        w1 = C(g("w_ff1")[l])
        w2 = C(g("w_ff2")[l])
        rm = _run("Cmlp", [dict(x1T=np.asarray(rc[c]["x1T"]), gain2=gain2, w_ff1=w1, w_ff2=w2) for c in range(NCORES)])
        del w1, w2, rc
        xT = [np.asarray(rm[c]["x2T"]) for c in range(NCORES)]
    out = np.empty((2, 4096, 4096), f32)
    for c in range(NCORES):
        out[c // 4, (c % 4) * 1024:(c % 4 + 1) * 1024, :] = xT[c].T
    return out
```
